# Optimizing a Trainium2 kernel written in Bass

```python
import math
import jax
import jax.numpy as jnp
from jax import lax
import numpy as np

D_MODEL = 4096
BATCH = 1
SEQ = 16384
DEPTH = 2
DEC_BATCH = 2
DEC_SEQ = 4096
PAST_LEN = 128

N_MIXERS = 2
N_HGRN_LAYERS = (DEPTH + 1) // 2
N_HYENA_LAYERS = DEPTH // 2
EPS = 1e-6
HG_HEAD_DIM = 128
HG_HEADS = D_MODEL // HG_HEAD_DIM
HG_CHUNK = 64
HG_STREAMS = 5
HY_ORDER = 2
HY_STREAMS = HY_ORDER + 1
HY_EMB_DIM = 33
HY_BANDS = (HY_EMB_DIM - 1) // 2
HY_FILTER_WIDTH = 64
HY_SHORT_CONV = 3
HY_TARGET = 1e-2
HY_FAST_DECAY_PCT = 0.3
HY_SLOW_DECAY_PCT = 1.5
HY_MAX_DECAY = math.log(HY_TARGET) / HY_FAST_DECAY_PCT
HY_MIN_DECAY = math.log(HY_TARGET) / HY_SLOW_DECAY_PCT
D_FF = -(-(8 * D_MODEL) // (3 * 256)) * 256

kernel_name = 'hgrn2_hyena_bidir_encoder'


def _rms_norm(x, gain):
    xf = x.astype(jnp.float32)
    y = xf * lax.rsqrt(jnp.mean(xf * xf, axis=-1, keepdims=True) + EPS)
    return (y * gain.astype(jnp.float32)).astype(x.dtype)


def _swiglu(h, w_gate, w_up, w_down):
    return (jax.nn.silu(h @ w_gate) * (h @ w_up)) @ w_down


def _hgrn2_scan(q, k, v, log_f):
    b_sz, t_len, n_h, d_k = q.shape
    d_v = v.shape[-1]
    n_chunks = t_len // HG_CHUNK

    def to_chunks(a):
        return a.reshape(b_sz, n_chunks, HG_CHUNK, n_h, a.shape[-1]).transpose(1, 0, 2, 3, 4)

    causal = jnp.tril(jnp.ones((HG_CHUNK, HG_CHUNK), dtype=bool))[None, :, :, None, None]

    def step(state, inp):
        qc, kc, vc, gc = inp
        cum = jnp.cumsum(gc, axis=1)
        o_inter = jnp.einsum('bthk,bhkv->bthv', qc * jnp.exp(cum), state)
        diff = cum[:, :, None] - cum[:, None, :]
        decay = jnp.where(causal, jnp.exp(jnp.minimum(diff, 0.0)), 0.0)
        scores = jnp.einsum('bthk,bshk,btshk->bhts', qc, kc, decay)
        o_intra = jnp.einsum('bhts,bshv->bthv', scores, vc)
        last = cum[:, -1]
        k_to_end = kc * jnp.exp(last[:, None] - cum)
        state = jnp.exp(last)[..., None] * state + jnp.einsum('bshk,bshv->bhkv', k_to_end, vc)
        return state, o_inter + o_intra

    state0 = jnp.zeros((b_sz, n_h, d_k, d_v), jnp.float32)
    _, out = lax.scan(step, state0, (to_chunks(q), to_chunks(k), to_chunks(v), to_chunks(log_f)))
    return out.transpose(1, 0, 2, 3, 4).reshape(b_sz, t_len, n_h, d_v)


def _hgrn2_mixer(h, w_in, lb, out_norm, w_out):
    b_sz, t_len, _ = h.shape
    proj = (h @ w_in).astype(jnp.float32)
    q, fl_fwd, fl_bwd, inp, gate = jnp.split(proj, HG_STREAMS, axis=-1)

    def heads(a):
        return a.reshape(b_sz, t_len, HG_HEADS, HG_HEAD_DIM)

    def rev(a):
        return jnp.flip(a, axis=1)

    q = jax.nn.silu(heads(q))
    v = heads(inp)
    lb = lb.astype(jnp.float32).reshape(2, HG_HEADS, HG_HEAD_DIM)

    def forget(logits, bound):
        f = bound + (1.0 - bound) * jax.nn.sigmoid(heads(logits))
        return 1.0 - f, jnp.log(f)

    k_f, lf_f = forget(fl_fwd, lb[0])
    k_b, lf_b = forget(fl_bwd, lb[1])
    o = _hgrn2_scan(q, k_f, v, lf_f) + rev(_hgrn2_scan(rev(q), rev(k_b), rev(v), rev(lf_b)))
    o = _rms_norm(o, out_norm.reshape(HG_HEADS, HG_HEAD_DIM)).reshape(b_sz, t_len, D_MODEL)
    o = o * jax.nn.silu(gate)
    return o.astype(h.dtype) @ w_out


def _hyena_mixer(h, w_in, conv_w, conv_b, fc1_w, fc1_b, fc2_w, fc2_b, fc3_w, fc3_b,
                 fc4_w, sin_freq, skip, w_out):
    b_sz, length, _ = h.shape
    f32 = jnp.float32
    u = h @ w_in
    up = jnp.pad(u, ((0, 0), (1, 1), (0, 0)))
    u = up[:, :-2] * conv_w[0] + up[:, 1:-1] * conv_w[1] + up[:, 2:] * conv_w[2] + conv_b
    v, x1, x2 = jnp.split(u.astype(f32), HY_STREAMS, axis=-1)
    t = jnp.linspace(0.0, 1.0, length, dtype=f32)[:, None]
    bands = jnp.linspace(1e-4, HY_BANDS - 1, HY_BANDS, dtype=f32)[None, :]
    ang = (2.0 * math.pi / length) * jnp.arange(length, dtype=f32)[:, None] * bands
    pos = jnp.concatenate([t, jnp.cos(ang), -jnp.sin(ang)], axis=-1)
    freq = sin_freq.astype(f32)
    z = jnp.sin(freq * (pos @ fc1_w.astype(f32) + fc1_b.astype(f32)))
    z = jnp.sin(freq * (z @ fc2_w.astype(f32) + fc2_b.astype(f32)))
    z = jnp.sin(freq * (z @ fc3_w.astype(f32) + fc3_b.astype(f32)))
    deltas = jnp.abs(jnp.linspace(HY_MIN_DECAY, HY_MAX_DECAY, D_MODEL, dtype=f32))
    window = jnp.exp(-t * deltas)[:, None, :]
    fc4 = fc4_w.astype(f32).reshape(HY_FILTER_WIDTH, HY_ORDER, 2 * D_MODEL)
    n_fft = 2 * length
    y = v
    for n, g in enumerate((x1, x2)):
        filt = (z @ fc4[:, n]).reshape(length, 2, D_MODEL) * window
        kern = jnp.concatenate([filt[:, 0], jnp.zeros((1, D_MODEL), f32), filt[:0:-1, 1]], axis=0)
        kern = kern / (jnp.sum(jnp.abs(kern), axis=0, keepdims=True) + EPS)
        spec = jnp.fft.rfft(y, n=n_fft, axis=1) * jnp.fft.rfft(kern, axis=0)[None]
        conv = jnp.fft.irfft(spec, n=n_fft, axis=1)[:, :length]
        y = g * (conv + y * skip[n].astype(f32))
    return y.astype(h.dtype) @ w_out


def _trunk(x, norm_mix, norm_ffn, norm_final, hg_w_in, hg_lb_logits, hg_out_norm, hg_w_out,
           hy_w_in, hy_conv_w, hy_conv_b, hy_fc1_w, hy_fc1_b, hy_fc2_w, hy_fc2_b, hy_fc3_w,
           hy_fc3_b, hy_fc4_w, hy_sin_freq, hy_skip, hy_w_out, ffn_w_gate, ffn_w_up, ffn_w_down):
    lb_all = jnp.cumsum(jax.nn.softmax(hg_lb_logits.astype(jnp.float32), axis=1), axis=1)
    for layer in range(DEPTH):
        slot = layer // N_MIXERS
        h = _rms_norm(x, norm_mix[layer])
        if layer % N_MIXERS == 0:
            x = x + _hgrn2_mixer(h, hg_w_in[slot], lb_all[:, slot], hg_out_norm[slot], hg_w_out[slot])
        else:
            x = x + _hyena_mixer(h, hy_w_in[slot], hy_conv_w[slot], hy_conv_b[slot],
                                 hy_fc1_w[slot], hy_fc1_b[slot], hy_fc2_w[slot], hy_fc2_b[slot],
                                 hy_fc3_w[slot], hy_fc3_b[slot], hy_fc4_w[slot], hy_sin_freq[slot],
                                 hy_skip[slot], hy_w_out[slot])
        x = x + _swiglu(_rms_norm(x, norm_ffn[layer]), ffn_w_gate[layer], ffn_w_up[layer], ffn_w_down[layer])
    return _rms_norm(x, norm_final)


def setup_inputs(seed: int = 0) -> dict:
    key = jax.random.key(seed)
    k = jax.random.split(key, 25)

    def nrm(kk, shape, scale):
        return scale * jax.random.normal(kk, shape, jnp.float32)

    d = D_MODEL
    na, nb = N_HGRN_LAYERS, N_HYENA_LAYERS
    wf = HY_FILTER_WIDTH
    return {
        'x_prompt': nrm(k[0], (BATCH, SEQ, d), 1.0),
        'x_sample': nrm(k[1], (DEC_BATCH, DEC_SEQ, d), 1.0),
        'norm_mix': 1.0 + nrm(k[2], (DEPTH, d), 0.02),
        'norm_ffn': 1.0 + nrm(k[3], (DEPTH, d), 0.02),
        'norm_final': 1.0 + nrm(k[4], (d,), 0.02),
        'hg_w_in': nrm(k[5], (na, d, HG_STREAMS * d), d ** -0.5),
        'hg_lb_logits': nrm(k[6], (2, na + 1, d), 0.5),
        'hg_out_norm': 1.0 + nrm(k[7], (na, d), 0.02),
        'hg_w_out': nrm(k[8], (na, d, d), d ** -0.5),
        'hy_w_in': nrm(k[9], (nb, d, HY_STREAMS * d), d ** -0.5),
        'hy_conv_w': nrm(k[10], (nb, HY_SHORT_CONV, HY_STREAMS * d), HY_SHORT_CONV ** -0.5),
        'hy_conv_b': nrm(k[11], (nb, HY_STREAMS * d), 0.02),
        'hy_fc1_w': nrm(k[12], (nb, HY_EMB_DIM, wf), HY_EMB_DIM ** -0.5),
        'hy_fc1_b': nrm(k[13], (nb, wf), 0.02),
        'hy_fc2_w': nrm(k[14], (nb, wf, wf), wf ** -0.5),
        'hy_fc2_b': nrm(k[15], (nb, wf), 0.02),
        'hy_fc3_w': nrm(k[16], (nb, wf, wf), wf ** -0.5),
        'hy_fc3_b': nrm(k[17], (nb, wf), 0.02),
        'hy_fc4_w': nrm(k[18], (nb, wf, HY_ORDER * 2 * d), wf ** -0.5),
        'hy_sin_freq': 1.0 + nrm(k[19], (nb, wf), 0.1),
        'hy_skip': nrm(k[20], (nb, HY_ORDER, d), 1.0),
        'hy_w_out': nrm(k[21], (nb, d, d), d ** -0.5),
        'ffn_w_gate': nrm(k[22], (DEPTH, d, D_FF), d ** -0.5),
        'ffn_w_up': nrm(k[23], (DEPTH, d, D_FF), d ** -0.5),
        'ffn_w_down': nrm(k[24], (DEPTH, D_FF, d), D_FF ** -0.5),
    }


def reference(x_prompt, x_sample, norm_mix, norm_ffn, norm_final, hg_w_in, hg_lb_logits,
              hg_out_norm, hg_w_out, hy_w_in, hy_conv_w, hy_conv_b, hy_fc1_w, hy_fc1_b,
              hy_fc2_w, hy_fc2_b, hy_fc3_w, hy_fc3_b, hy_fc4_w, hy_sin_freq, hy_skip, hy_w_out,
              ffn_w_gate, ffn_w_up, ffn_w_down):
    params = (norm_mix, norm_ffn, norm_final, hg_w_in, hg_lb_logits, hg_out_norm, hg_w_out,
              hy_w_in, hy_conv_w, hy_conv_b, hy_fc1_w, hy_fc1_b, hy_fc2_w, hy_fc2_b, hy_fc3_w,
              hy_fc3_b, hy_fc4_w, hy_sin_freq, hy_skip, hy_w_out, ffn_w_gate, ffn_w_up, ffn_w_down)
    y_prompt = _trunk(x_prompt, *params)
    y_sample = _trunk(x_sample, *params)
    return (y_prompt, y_sample)
```

```python
import contextlib
import math
import numpy as np
import concourse.bass as bass
import concourse.mybir as mybir
from concourse.bass_utils import run_bass_kernel_spmd

F32 = mybir.dt.float32
BF16 = mybir.dt.bfloat16
AF = mybir.ActivationFunctionType
ALU = mybir.AluOpType
AX = mybir.AxisListType
EPS = 1e-6


class DSem:
    def __init__(self, sem, inc=16):
        self.sem = sem
        self.cnt = 0
        self.inc = inc


class Buf:
    __slots__ = ("w", "r", "name")

    def __init__(self, name=""):
        self.w = {}
        self.r = {}
        self.name = name


class Sync:
    def __init__(self, nc, stack):
        self.nc = nc
        self.stack = stack
        self.eng = {"pe": nc.tensor, "act": nc.scalar, "dve": nc.vector,
                    "pool": nc.gpsimd, "sp": nc.sync}
        self.sem = {k: stack.enter_context(nc.semaphore("s_" + k)) for k in self.eng}
        self.cnt = {k: 0 for k in self.eng}
        self.seen = {k: {} for k in self.eng}
        self.nsem = 0
        self.all_dsems = []

    def dsem(self, inc=16):
        free = getattr(self, "free_dsems", None)
        if free is None:
            self.free_dsems = free = []
        for i, d in enumerate(free):
            if d.inc == inc:
                free.pop(i)
                self.live.append(d)
                return d
        self.nsem += 1
        d = DSem(self.stack.enter_context(self.nc.semaphore("d%d" % self.nsem)), inc)
        self.all_dsems.append(d)
        if not hasattr(self, "live"):
            self.live = []
        self.live.append(d)
        return d

    def release_phase(self, keep=()):
        live = getattr(self, "live", [])
        self.free_dsems.extend(d for d in live if d not in keep)
        self.live = [d for d in live if d in keep]

    def _wait(self, e, reads, writes):
        deps = {}
        for b in reads:
            for s, c in b.w.items():
                if deps.get(s, 0) < c:
                    deps[s] = c
        for b in writes:
            for s, c in b.w.items():
                if deps.get(s, 0) < c:
                    deps[s] = c
            for s, c in b.r.items():
                if deps.get(s, 0) < c:
                    deps[s] = c
        seen = self.seen[e]
        for s, c in deps.items():
            if isinstance(s, DSem):
                c = s.cnt
                if seen.get(s, 0) >= c:
                    continue
                self.eng[e].wait_ge(s.sem, c * s.inc)
                seen[s] = c
            else:
                if s == e and e == "pe":
                    continue
                if seen.get(s, 0) >= c:
                    continue
                self.eng[e].wait_ge(self.sem[s], c)
                seen[s] = c

    def op(self, e, fn, reads=(), writes=()):
        self._wait(e, reads, writes)
        ins = fn(self.eng[e])
        self.cnt[e] += 1
        c = self.cnt[e]
        ins.then_inc(self.sem[e], 1)
        for b in reads:
            b.r[e] = c
        for b in writes:
            b.w = {e: c}
            b.r = {}
        return ins

    def dma(self, e, d, fn, reads=(), writes=()):
        self._wait(e, reads, writes)
        ins = fn(self.eng[e])
        d.cnt += 1
        ins.then_inc(d.sem, d.inc)
        for b in reads:
            b.r[d] = d.cnt
        for b in writes:
            b.w = {d: d.cnt}
            b.r = {}
        return ins

    def finish(self):
        e = "sp"
        for d in self.all_dsems:
            if d.cnt > 0 and self.seen[e].get(d, 0) < d.cnt:
                self.eng[e].wait_ge(d.sem, d.cnt * d.inc)
        for k in self.eng:
            if k != e and self.cnt[k] > 0:
                self.eng[e].wait_ge(self.sem[k], self.cnt[k])


class Ring:
    def __init__(self, tiles, dsems=None):
        self.tiles = tiles
        self.bufs = [Buf() for _ in tiles]
        self.dsems = dsems
        self.i = 0

    def next(self):
        j = self.i % len(self.tiles)
        self.i += 1
        if self.dsems is not None:
            return self.tiles[j], self.bufs[j], self.dsems[j]
        return self.tiles[j], self.bufs[j]


class Ctx:
    def __init__(self, nc, stack, cfg):
        self.nc = nc
        self.stack = stack
        self.cfg = cfg
        self.S = Sync(nc, stack)

    def sb(self, name, shape, dt):
        return self.stack.enter_context(self.nc.sbuf_tensor("sb_" + name, shape, dt))

    def ps(self, name, shape, dt):
        return self.stack.enter_context(self.nc.psum_tensor("ps_" + name, shape, dt))

    def ring_sb(self, name, n, shape, dt, dma=False):
        tiles = [self.sb("%s%d" % (name, i), shape, dt) for i in range(n)]
        ds = [self.S.dsem() for _ in range(n)] if dma else None
        return Ring(tiles, ds)


class Dense:
    def __init__(self, cx, D, DFF, TT, pbank=None):
        self.cx = cx
        nc = cx.nc
        S = cx.S
        self.D, self.DFF, self.TT = D, DFF, TT
        self.KT = D // 128
        self.FT = DFF // 128
        self.OBW = min(512, D)
        self.OB = D // self.OBW
        self.NT = TT // 128
        KT, FT, NT = self.KT, self.FT, self.NT
        hid_elems = FT * TT
        need = KT * TT + 2 * D + D
        self.arena = cx.sb("arena", [128, max(hid_elems, need)], BF16)
        self.hid = self.arena[:, 0:hid_elems].rearrange("p (f t) -> p f t", f=FT)
        self.oT = self.arena[:, 0:KT * TT].rearrange("p (k t) -> p k t", k=KT)
        self.xt = self.arena[:, KT * TT:KT * TT + 2 * D].bitcast(F32)
        self.hb = self.arena[:, KT * TT + 2 * D:KT * TT + 3 * D]
        self.b_arena = Buf("arena")
        self.b_oT = Buf("oT")
        self.b_xt = Buf("xt")
        self.b_hb = Buf("hb")
        self.b_hid = [Buf("hid%d" % f) for f in range(FT)]
        self.hT = cx.sb("hT", [128, KT, TT], BF16)
        self.b_hT = [Buf("hT%d" % t) for t in range(NT)]
        self.gain = cx.sb("gain", [128, D], F32)
        self.b_gain = Buf("gain")
        self.d_gain = S.dsem()
        self.d_misc = S.dsem()
        self.wgu = cx.ring_sb("wgu", 2, [128, KT, 256], BF16, dma=True)
        self.wrow = cx.ring_sb("wrow", 4, [128, self.OBW], BF16, dma=True)
        self.xp = cx.ring_sb("xp", 4, [128, self.OBW], F32, dma=True)
        self.sq = cx.ring_sb("sq", 2, [128, self.OBW], F32)
        self.sg = cx.ring_sb("sg", 2, [128, TT], F32)
        self.ssq = cx.sb("ssq", [128, NT, self.OB], F32)
        self.b_ssq = [[Buf() for _ in range(self.OB)] for _ in range(NT)]
        self.rstd = cx.sb("rstd", [128, NT], F32)
        self.b_rstd = [Buf() for _ in range(NT)]
        self.ident = cx.sb("ident", [128, 128], BF16)
        self.b_ident = Buf()
        self.pbank = pbank if pbank is not None else Ring([cx.ps("pb%d" % i, [128, 512], F32) for i in range(8)])

    def load_consts(self, ident_dram):
        S = self.cx.S
        S.dma("sp", self.d_misc, lambda e: e.dma_start(out=self.ident[:], in_=ident_dram),
              writes=[self.b_ident])

    def load_gain(self, gain_dram):
        S = self.cx.S
        S.dma("sp", self.d_gain,
              lambda e: e.dma_start(out=self.gain[:], in_=gain_dram.partition_broadcast(128)),
              writes=[self.b_gain])

    def fence(self, olds, news):
        for nb in news:
            for ob_ in olds:
                for s_, c_ in list(ob_.w.items()) + list(ob_.r.items()):
                    if nb.r.get(s_, 0) < c_:
                        nb.r[s_] = c_

    def proj_residual(self, actT, act_bufs, nk, w_dram, x_src, x_dst, sbufs, xbufs):
        S = self.cx.S
        NT, OB, OBW = self.NT, self.OB, self.OBW
        for tt in range(NT):
            S.op("dve", lambda e, tt=tt: e.memset(self.ssq[:, tt, :], 0.0), writes=list(self.b_ssq[tt]))
        for ob in range(OB):
            banks = [self.pbank.next() for _ in range(NT)]
            for k in range(nk):
                wt, wb, wd = self.wrow.next()
                S.dma("pool", wd, lambda e, wt=wt, k=k, ob=ob: e.dma_start(
                    out=wt[:], in_=w_dram[k * 128:(k + 1) * 128, ob * OBW:(ob + 1) * OBW]),
                    writes=[wb])

                def mm(e, wt=wt, k=k):
                    ins = None
                    for tt in range(NT):
                        ins = e.matmul(banks[tt][0][:, 0:OBW], actT[:, k, tt * 128:(tt + 1) * 128],
                                       wt[:], start=(k == 0), stop=(k == nk - 1))
                    return ins
                rb = [wb] + (act_bufs if isinstance(act_bufs, list) else [act_bufs])
                S.op("pe", mm, reads=rb, writes=[b for _, b in banks])
            for tt in range(NT):
                pt, pb = banks[tt]
                xt, xb, xd = self.xp.next()
                S.dma("sp", xd, lambda e, xt=xt, tt=tt, ob=ob: e.dma_start(
                    out=xt[:], in_=x_src[tt * 128:(tt + 1) * 128, ob * OBW:(ob + 1) * OBW]),
                    reads=([sbufs[tt][ob]] if sbufs is not None else []), writes=[xb])
                S.op("dve", lambda e, xt=xt, pt=pt: e.tensor_tensor(
                    out=xt[:], in0=pt[:, 0:OBW], in1=xt[:], op=ALU.add),
                    reads=[pb], writes=[xb])
                sq, sqb = self.sq.next()
                S.op("act", lambda e, xt=xt, sq=sq, tt=tt, ob=ob: e.activation(
                    out=sq[:], in_=xt[:], func=AF.Square,
                    accum_out=self.ssq[:, tt, ob:ob + 1]),
                    reads=[xb], writes=[sqb, self.b_ssq[tt][ob]])
                S.dma("sp", xd, lambda e, xt=xt, tt=tt, ob=ob: e.dma_start(
                    out=x_dst[tt * 128:(tt + 1) * 128, ob * OBW:(ob + 1) * OBW], in_=xt[:]),
                    reads=[xb], writes=[xbufs[tt][ob]])

    def calc_rstd(self, tt):
        S = self.cx.S
        D = self.D
        r = self.rstd[:, tt:tt + 1]
        S.op("dve", lambda e: e.tensor_reduce(out=r, in_=self.ssq[:, tt, :], axis=AX.X, op=ALU.add),
             reads=list(self.b_ssq[tt]), writes=[self.b_rstd[tt]])
        S.op("dve", lambda e: e.tensor_scalar(out=r, in0=r, scalar1=1.0 / D, scalar2=EPS,
                                              op0=ALU.mult, op1=ALU.add),
             reads=[self.b_rstd[tt]], writes=[self.b_rstd[tt]])
        S.op("act", lambda e: e.activation(out=r, in_=r, func=AF.Ln),
             reads=[self.b_rstd[tt]], writes=[self.b_rstd[tt]])
        S.op("act", lambda e: e.activation(out=r, in_=r, func=AF.Exp, scale=-0.5),
             reads=[self.b_rstd[tt]], writes=[self.b_rstd[tt]])

    def sumsq_tile(self, tt):
        S = self.cx.S
        S.op("dve", lambda e: e.memset(self.ssq[:, tt, :], 0.0), writes=list(self.b_ssq[tt]))
        for ob in range(self.OB):
            sq, sqb = self.sq.next()
            S.op("act", lambda e, sq=sq, ob=ob: e.activation(
                out=sq[:], in_=self.xt[:, ob * self.OBW:(ob + 1) * self.OBW], func=AF.Square,
                accum_out=self.ssq[:, tt, ob:ob + 1]),
                reads=[self.b_xt], writes=[sqb, self.b_ssq[tt][ob]])

    def norm_tile_to_hT(self, tt, x_dram_tile, xbufs_tt, preloaded=False):
        S = self.cx.S
        KT = self.KT
        if not preloaded:
            S.dma("sp", self.d_misc, lambda e: e.dma_start(out=self.xt[:], in_=x_dram_tile),
                  reads=list(xbufs_tt), writes=[self.b_xt])
        S.op("dve", lambda e: e.scalar_tensor_tensor(
            out=self.hb[:], in0=self.xt[:], scalar=self.rstd[:, tt:tt + 1], in1=self.gain[:],
            op0=ALU.mult, op1=ALU.mult),
            reads=[self.b_xt, self.b_rstd[tt], self.b_gain], writes=[self.b_hb])
        k0 = 0
        while k0 < KT:
            n = min(8, KT - k0)
            pt, pb = self.pbank.next()
            ptb = pt[:].bitcast(BF16)

            def tr(e, k0=k0, n=n, ptb=ptb):
                ins = None
                for j in range(n):
                    ins = e.transpose(ptb[:, j * 128:(j + 1) * 128],
                                      self.hb[:, (k0 + j) * 128:(k0 + j + 1) * 128], self.ident[:])
                return ins
            S.op("pe", tr, reads=[self.b_hb, self.b_ident], writes=[pb])
            S.op("act", lambda e, k0=k0, n=n, ptb=ptb: e.copy(
                out=self.hT[:, k0:k0 + n, tt * 128:(tt + 1) * 128],
                in_=ptb[:, 0:n * 128].rearrange("p (k t) -> p k t", k=n)),
                reads=[pb], writes=[self.b_hT[tt]])
            k0 += n

    def norm_tile_to_out(self, tt, x_dram_tile, xbufs_tt, y_dram_tile, ybuf):
        S = self.cx.S
        S.dma("sp", self.d_misc, lambda e: e.dma_start(out=self.xt[:], in_=x_dram_tile),
              reads=list(xbufs_tt), writes=[self.b_xt])
        S.op("dve", lambda e: e.scalar_tensor_tensor(
            out=self.xt[:], in0=self.xt[:], scalar=self.rstd[:, tt:tt + 1], in1=self.gain[:],
            op0=ALU.mult, op1=ALU.mult),
            reads=[self.b_xt, self.b_rstd[tt], self.b_gain], writes=[self.b_xt])
        S.dma("sp", self.d_misc, lambda e: e.dma_start(out=y_dram_tile, in_=self.xt[:]),
              reads=[self.b_xt], writes=[ybuf])

    def gate_up(self, wg_dram, wu_dram):
        S = self.cx.S
        KT, FT, TT = self.KT, self.FT, self.TT
        wgv = wg_dram.rearrange("(k p) f -> p k f", p=128)
        wuv = wu_dram.rearrange("(k p) f -> p k f", p=128)
        for ft in range(FT):
            wt, wb, wd = self.wgu.next()
            S.dma("pool", wd, lambda e, wt=wt, ft=ft: e.dma_start(
                out=wt[:, :, 0:128], in_=wgv[:, :, ft * 128:(ft + 1) * 128]), writes=[wb])
            S.dma("pool", wd, lambda e, wt=wt, ft=ft: e.dma_start(
                out=wt[:, :, 128:256], in_=wuv[:, :, ft * 128:(ft + 1) * 128]), writes=[wb])
            pg, pgb = self.pbank.next()
            pu, pub = self.pbank.next()

            def mm(e, wt=wt, pg=pg, pu=pu):
                ins = None
                for k in range(KT):
                    ins = e.matmul(pg[:, 0:TT], wt[:, k, 0:128], self.hT[:, k, :],
                                   start=(k == 0), stop=(k == KT - 1))
                for k in range(KT):
                    ins = e.matmul(pu[:, 0:TT], wt[:, k, 128:256], self.hT[:, k, :],
                                   start=(k == 0), stop=(k == KT - 1))
                return ins
            S.op("pe", mm, reads=[wb] + self.b_hT, writes=[pgb, pub])
            sg, sgb = self.sg.next()
            S.op("act", lambda e, sg=sg, pg=pg: e.activation(out=sg[:], in_=pg[:, 0:TT], func=AF.Silu),
                 reads=[pgb], writes=[sgb])
            S.op("dve", lambda e, sg=sg, pu=pu, ft=ft: e.tensor_tensor(
                out=self.hid[:, ft, :], in0=pu[:, 0:TT], in1=sg[:], op=ALU.mult),
                reads=[pub, sgb], writes=[self.b_hid[ft]])

    def group(self, oT_src, x_src, sbufs, xres, xbufs, wo, wg, wu, wd, gain_ffn, gain_next,
              hT_dst=None, hT_dst_buf=None, y_dst=None, y_buf=None):
        S = self.cx.S
        NT, KT, FT = self.NT, self.KT, self.FT
        stage1 = [self.b_oT, self.b_xt, self.b_hb]
        self.fence(self.b_hid, stage1)
        self.load_gain(gain_ffn)
        S.dma("sp", self.d_misc, lambda e: e.dma_start(out=self.oT, in_=oT_src), writes=[self.b_oT])
        self.proj_residual(self.oT, self.b_oT, KT, wo, x_src, xres, sbufs, xbufs)
        for tt in range(NT):
            self.calc_rstd(tt)
            self.norm_tile_to_hT(tt, xres[tt * 128:(tt + 1) * 128, :], xbufs[tt])
        self.fence(stage1, self.b_hid)
        self.gate_up(wg, wu)
        self.load_gain(gain_next)
        self.proj_residual(self.hid, self.b_hid, FT, wd, xres, xres, xbufs, xbufs)
        self.fence(self.b_hid, stage1)
        for tt in range(NT):
            self.calc_rstd(tt)
            if y_dst is None:
                self.norm_tile_to_hT(tt, xres[tt * 128:(tt + 1) * 128, :], xbufs[tt])
            else:
                self.norm_tile_to_out(tt, xres[tt * 128:(tt + 1) * 128, :], xbufs[tt],
                                      y_dst[tt * 128:(tt + 1) * 128, :], y_buf)
        if y_dst is None:
            S.dma("sp", self.d_misc, lambda e: e.dma_start(out=hT_dst, in_=self.hT[:]),
                  reads=self.b_hT, writes=[hT_dst_buf])

    def norm_group(self, x_src, gain_loaded, hT_dst, hT_dst_buf):
        S = self.cx.S
        for tt in range(self.NT):
            S.dma("sp", self.d_misc, lambda e, tt=tt: e.dma_start(
                out=self.xt[:], in_=x_src[tt * 128:(tt + 1) * 128, :]), writes=[self.b_xt])
            self.sumsq_tile(tt)
            self.calc_rstd(tt)
            self.norm_tile_to_hT(tt, None, None, preloaded=True)
        S.dma("sp", self.d_misc, lambda e: e.dma_start(out=hT_dst, in_=self.hT[:]),
              reads=self.b_hT, writes=[hT_dst_buf])


CH = 64
MID = CH // 2 - 1


def hgrn_consts():
    C = CH
    s = np.arange(C)[:, None]
    t = np.arange(C)[None, :]
    out = {}
    for name, fwd in (("f", True), ("b", False)):
        if fwd:
            tri = (s <= t).astype(np.float32)
            mid = (s[:, 0] <= MID).astype(np.float32)
        else:
            tri = (s >= t).astype(np.float32)
            mid = (s[:, 0] >= C - 1 - MID).astype(np.float32)
        D = tri - mid[:, None]
        ext = np.concatenate([D, mid[:, None], (1 - mid)[:, None], np.ones((C, 1), np.float32)], 1)
        W = C + 3
        ext2 = np.zeros((128, 2 * W), np.float32)
        ext2[0:C, 0:W] = ext
        ext2[C:2 * C, W:2 * W] = ext
        DD = np.zeros((128, 128), np.float32)
        DD[0:C, 0:C] = D
        DD[C:, C:] = D
        m2 = np.concatenate([tri, tri], 0)
        out["hc_ext_" + name] = ext2
        out["hc_dd_" + name] = DD
        out["hc_mask_" + name] = m2
    out["hc_ones"] = np.ones((128, 128), np.float32)
    return out


class Hgrn:
    def __init__(self, cx, D, GT, LMAX, pbank):
        self.cx = cx
        S = cx.S
        self.D, self.GT = D, GT
        self.KT = D // 128
        self.NT = GT // 128
        KT, NT = self.KT, self.NT
        W = CH + 3
        self.W = W
        self.pbank = pbank
        sb = cx.sb
        self.w = sb("hg_w", [128, KT, 640], BF16)
        self.b_w = Buf()
        self.d_w = S.dsem()
        self.d_misc = S.dsem()
        self.ext = {d: sb("hg_ext" + d, [128, 2 * W], F32) for d in "fb"}
        self.dd = {d: sb("hg_dd" + d, [128, 128], F32) for d in "fb"}
        self.mask = {d: sb("hg_mask" + d, [128, CH], F32) for d in "fb"}
        self.ones = sb("hg_ones", [128, 128], F32)
        self.b_const = Buf()
        self.lgc = sb("hg_lgc", [128, 2, 2], F32)
        self.lbc = sb("hg_lbc", [128, 2], F32)
        self.omlc = sb("hg_omlc", [128, 2], F32)
        self.nomlc = sb("hg_nomlc", [128, 2], F32)
        self.lgb = sb("hg_lgb", [128, 2, 2, 128], F32)
        self.lbb = sb("hg_lbb", [128, 2, 128], F32)
        self.omlb = sb("hg_omlb", [128, 2, 128], F32)
        self.gcol = sb("hg_gcol", [128, 1], F32)
        self.b_lb = Buf()
        self.hT = cx.ring_sb("hg_hT", 2, [128, KT, GT], BF16, dma=True)
        self.qT = sb("hg_qT", [128, GT], F32); self.b_qT = Buf()
        self.kT = sb("hg_kT", [128, GT], F32); self.b_kT = Buf()
        self.gT = sb("hg_gT", [128, GT], F32); self.b_gT = Buf()
        self.sT = sb("hg_sT", [128, GT], F32); self.b_sT = Buf()
        self.ftok = sb("hg_ftok", [128, NT, 128], F32); self.b_ftok = [Buf() for _ in range(NT)]
        self.logf = sb("hg_logf", [128, NT, 128], F32); self.b_logf = [Buf() for _ in range(NT)]
        self.ktok = sb("hg_ktok", [128, NT, 128], F32); self.b_ktok = [Buf() for _ in range(NT)]
        self.vtok = sb("hg_vtok", [128, NT, 128], BF16); self.b_vtok = [Buf() for _ in range(NT)]
        self.E1 = sb("hg_E1", [128, NT, 2, CH], F32); self.b_E1 = [Buf() for _ in range(NT)]
        self.E2 = sb("hg_E2", [128, NT, 2, CH], F32); self.b_E2 = [Buf() for _ in range(NT)]
        self.E2t = sb("hg_E2t", [128, NT, 128], F32); self.b_E2t = [Buf() for _ in range(NT)]
        self.em = sb("hg_em", [128, NT, 2, 3], F32); self.b_em = [Buf() for _ in range(NT)]
        self.qs = sb("hg_qs", [128, GT], BF16); self.b_qs = [Buf() for _ in range(NT)]
        self.ks = sb("hg_ks", [128, GT], BF16); self.b_ks = [Buf() for _ in range(NT)]
        self.kst = sb("hg_kst", [128, NT, 128], BF16); self.b_kst = [Buf() for _ in range(NT)]
        self.P = cx.ring_sb("hg_P", 2, [128, 128], BF16)
        self.St = sb("hg_S", [128, 128], F32); self.b_S = Buf()
        self.S2 = sb("hg_S2", [128, 128], F32); self.b_S2 = Buf()
        self.Sp = cx.ring_sb("hg_Sp", 2, [128, 128], BF16)
        self.ofw = sb("hg_ofw", [128, LMAX], BF16); self.b_ofw = Buf()
        self.osum = sb("hg_osum", [128, 128], F32); self.b_osum = Buf()
        self.osq = sb("hg_osq", [128, 128], F32); self.b_osq = Buf()
        self.rs = sb("hg_rs", [128, 128], F32); self.b_rs = Buf()
        self.oout = cx.ring_sb("hg_oout", 2, [128, GT], BF16, dma=True)

    def load_consts(self, cd):
        S = self.cx.S
        for d in "fb":
            S.dma("sp", self.d_misc, lambda e, d=d: e.dma_start(out=self.ext[d][:], in_=cd["hc_ext_" + d]), writes=[self.b_const])
            S.dma("sp", self.d_misc, lambda e, d=d: e.dma_start(out=self.dd[d][:], in_=cd["hc_dd_" + d]), writes=[self.b_const])
            S.dma("sp", self.d_misc, lambda e, d=d: e.dma_start(out=self.mask[d][:], in_=cd["hc_mask_" + d]), writes=[self.b_const])
        S.dma("sp", self.d_misc, lambda e: e.dma_start(out=self.ones[:], in_=cd["hc_ones"]), writes=[self.b_const])
        for P_, pb in zip(self.P.tiles, self.P.bufs):
            S.op("dve", lambda e, P_=P_: e.memset(P_[:], 0.0), writes=[pb])

    def load_head(self, w_dram, lg_dram, gn_dram):
        S = self.cx.S
        S.dma("pool", self.d_w, lambda e: e.dma_start(
            out=self.w[:], in_=w_dram.rearrange("(k p) f -> p k f", p=128)), writes=[self.b_w])
        S.dma("sp", self.d_misc, lambda e: e.dma_start(
            out=self.lgc[:], in_=lg_dram.rearrange("d s c -> c d s"), allow_slow_non_contiguous=True),
            writes=[self.b_lb])
        S.dma("sp", self.d_misc, lambda e: e.dma_start(
            out=self.lgb[:].rearrange("p d s c -> p (d s c)"),
            in_=lg_dram.rearrange("d s c -> (d s c)").partition_broadcast(128)), writes=[self.b_lb])
        S.dma("sp", self.d_misc, lambda e: e.dma_start(
            out=self.gcol[:], in_=gn_dram.rearrange("(c o) -> c o", o=1)), writes=[self.b_lb])
        B = [self.b_lb]
        S.op("dve", lambda e: e.tensor_tensor(out=self.lbc[:], in0=self.lgc[:, :, 0], in1=self.lgc[:, :, 1], op=ALU.subtract), reads=B, writes=B)
        S.op("act", lambda e: e.activation(out=self.lbc[:], in_=self.lbc[:], func=AF.Sigmoid), reads=B, writes=B)
        S.op("dve", lambda e: e.tensor_scalar(out=self.omlc[:], in0=self.lbc[:], scalar1=-1.0, scalar2=1.0, op0=ALU.mult, op1=ALU.add), reads=B, writes=B)
        S.op("dve", lambda e: e.tensor_scalar(out=self.nomlc[:], in0=self.omlc[:], scalar1=-1.0, scalar2=None, op0=ALU.mult), reads=B, writes=B)
        S.op("dve", lambda e: e.tensor_tensor(out=self.lbb[:], in0=self.lgb[:, :, 0, :], in1=self.lgb[:, :, 1, :], op=ALU.subtract), reads=B, writes=B)
        S.op("act", lambda e: e.activation(out=self.lbb[:], in_=self.lbb[:], func=AF.Sigmoid), reads=B, writes=B)
        S.op("dve", lambda e: e.tensor_scalar(out=self.omlb[:], in0=self.lbb[:], scalar1=-1.0, scalar2=1.0, op0=ALU.mult, op1=ALU.add), reads=B, writes=B)

    def sweep(self, hT_src_fn, n_groups, fwd, o_dst_fn=None, o_dst_bufs=None):
        S = self.cx.S
        KT, NT, GT, W = self.KT, self.NT, self.GT, self.W
        d = "f" if fwd else "b"
        di = 0 if fwd else 1
        fcol = 128 if fwd else 384
        tcol = 128 if fwd else 256
        f_off, v_off = (0, 128) if fwd else (128, 0)
        S.op("dve", lambda e: e.memset(self.St[:], 0.0), writes=[self.b_S])
        groups = range(n_groups) if fwd else range(n_groups - 1, -1, -1)
        for g in groups:
            hT, hb, hd = self.hT.next()
            S.dma("sp", hd, lambda e, hT=hT, g=g: e.dma_start(out=hT[:], in_=hT_src_fn(g)), writes=[hb])
            def fm(col, pt):
                def f(e):
                    ins = None
                    for k in range(KT):
                        ins = e.matmul(pt[:, 0:GT], self.w[:, k, col:col + 128], hT[:, k, :],
                                       start=(k == 0), stop=(k == KT - 1))
                    return ins
                return f
            pq, pqb = self.pbank.next()
            S.op("pe", fm(0, pq), reads=[self.b_w, hb], writes=[pqb])
            S.op("act", lambda e, pq=pq: e.activation(out=self.qT[:], in_=pq[:, 0:GT], func=AF.Silu),
                 reads=[pqb], writes=[self.b_qT])
            pf, pfb = self.pbank.next()
            S.op("pe", fm(fcol, pf), reads=[self.b_w, hb], writes=[pfb])
            S.op("act", lambda e, pf=pf: e.activation(out=self.sT[:], in_=pf[:, 0:GT], func=AF.Sigmoid),
                 reads=[pfb], writes=[self.b_sT])
            S.op("dve", lambda e: e.tensor_scalar(out=self.kT[:], in0=self.sT[:],
                                                  scalar1=self.nomlc[:, di:di + 1], scalar2=self.omlc[:, di:di + 1],
                                                  op0=ALU.mult, op1=ALU.add),
                 reads=[self.b_sT, self.b_lb], writes=[self.b_kT])
            if not fwd:
                pg, pgb = self.pbank.next()
                S.op("pe", fm(512, pg), reads=[self.b_w, hb], writes=[pgb])
                S.op("act", lambda e, pg=pg: e.activation(out=self.gT[:], in_=pg[:, 0:GT], func=AF.Silu),
                     reads=[pgb], writes=[self.b_gT])
            for tt in range(NT):
                pt, ptb = self.pbank.next()

                def tm(e, pt=pt, tt=tt):
                    ins = None
                    for k in range(KT):
                        ins = e.matmul(pt[:, 0:256], hT[:, k, tt * 128:(tt + 1) * 128],
                                       self.w[:, k, tcol:tcol + 256], start=(k == 0), stop=(k == KT - 1))
                    return ins
                S.op("pe", tm, reads=[self.b_w, hb], writes=[ptb])
                S.op("act", lambda e, pt=pt, tt=tt: e.activation(
                    out=self.ftok[:, tt, :], in_=pt[:, f_off:f_off + 128], func=AF.Sigmoid),
                    reads=[ptb], writes=[self.b_ftok[tt]])
                S.op("act", lambda e, pt=pt, tt=tt: e.copy(out=self.vtok[:, tt, :], in_=pt[:, v_off:v_off + 128]),
                     reads=[ptb], writes=[self.b_vtok[tt]])
                S.op("dve", lambda e, tt=tt: e.tensor_tensor(out=self.ftok[:, tt, :], in0=self.ftok[:, tt, :],
                                                             in1=self.omlb[:, di, :], op=ALU.mult),
                     reads=[self.b_ftok[tt], self.b_lb], writes=[self.b_ftok[tt]])
                S.op("dve", lambda e, tt=tt: e.tensor_tensor(out=self.ftok[:, tt, :], in0=self.ftok[:, tt, :],
                                                             in1=self.lbb[:, di, :], op=ALU.add),
                     reads=[self.b_ftok[tt], self.b_lb], writes=[self.b_ftok[tt]])
                S.op("act", lambda e, tt=tt: e.activation(out=self.logf[:, tt, :], in_=self.ftok[:, tt, :], func=AF.Ln),
                     reads=[self.b_ftok[tt]], writes=[self.b_logf[tt]])
                S.op("dve", lambda e, tt=tt: e.tensor_scalar(out=self.ktok[:, tt, :], in0=self.ftok[:, tt, :],
                                                             scalar1=-1.0, scalar2=1.0, op0=ALU.mult, op1=ALU.add),
                     reads=[self.b_ftok[tt]], writes=[self.b_ktok[tt]])
                pr, prb = self.pbank.next()
                S.op("pe", lambda e, pr=pr, tt=tt: e.matmul(pr[:, 0:2 * W], self.logf[:, tt, :], self.ext[d][:],
                                                            start=True, stop=True),
                     reads=[self.b_logf[tt], self.b_const], writes=[prb])
                prv = pr[:, 0:2 * W].rearrange("p (c w) -> p c w", c=2)
                S.op("act", lambda e, prv=prv, tt=tt: e.activation(out=self.E1[:, tt, :, :], in_=prv[:, :, 0:CH], func=AF.Exp),
                     reads=[prb], writes=[self.b_E1[tt]])
                S.op("act", lambda e, prv=prv, tt=tt: e.activation(out=self.E2[:, tt, :, :], in_=prv[:, :, 0:CH], func=AF.Exp, scale=-1.0),
                     reads=[prb], writes=[self.b_E2[tt]])
                S.op("act", lambda e, prv=prv, tt=tt: e.activation(out=self.em[:, tt, :, :], in_=prv[:, :, CH:CH + 3], func=AF.Exp),
                     reads=[prb], writes=[self.b_em[tt]])
                pr2, pr2b = self.pbank.next()
                S.op("pe", lambda e, pr2=pr2, tt=tt: e.matmul(pr2[:, 0:128], self.dd[d][:], self.logf[:, tt, :],
                                                              start=True, stop=True),
                     reads=[self.b_logf[tt], self.b_const], writes=[pr2b])
                S.op("act", lambda e, pr2=pr2, tt=tt: e.activation(out=self.E2t[:, tt, :], in_=pr2[:, 0:128], func=AF.Exp, scale=-1.0),
                     reads=[pr2b], writes=[self.b_E2t[tt]])
                sl = slice(tt * 128, (tt + 1) * 128)
                S.op("dve", lambda e, tt=tt, sl=sl: e.tensor_tensor(
                    out=self.qs[:, sl], in0=self.qT[:, sl], in1=self.E1[:, tt, :, :].rearrange("p c w -> p (c w)"), op=ALU.mult),
                    reads=[self.b_qT, self.b_E1[tt]], writes=[self.b_qs[tt]])
                S.op("dve", lambda e, tt=tt, sl=sl: e.tensor_tensor(
                    out=self.ks[:, sl], in0=self.kT[:, sl], in1=self.E2[:, tt, :, :].rearrange("p c w -> p (c w)"), op=ALU.mult),
                    reads=[self.b_kT, self.b_E2[tt]], writes=[self.b_ks[tt]])
                S.op("dve", lambda e, tt=tt: e.tensor_tensor(
                    out=self.kst[:, tt, :], in0=self.ktok[:, tt, :], in1=self.E2t[:, tt, :], op=ALU.mult),
                    reads=[self.b_ktok[tt], self.b_E2t[tt]], writes=[self.b_kst[tt]])
            tts = range(NT) if fwd else range(NT - 1, -1, -1)
            oo, oob, ood = (None, None, None)
            if not fwd:
                oo, oob, ood = self.oout.next()
            for tt in tts:
                sl = slice(tt * 128, (tt + 1) * 128)
                psc, pscb = self.pbank.next()
                S.op("pe", lambda e, psc=psc, sl=sl: e.matmul(psc[:, 0:128], self.ks[:, sl], self.qs[:, sl], start=True, stop=True),
                     reads=[self.b_ks[tt], self.b_qs[tt]], writes=[pscb])
                P_, Pb = self.P.next()
                for h in range(2):
                    rs_ = slice(h * CH, (h + 1) * CH)
                    S.op("dve", lambda e, psc=psc, P_=P_, rs_=rs_: e.tensor_tensor(
                        out=P_[rs_, rs_], in0=psc[rs_, rs_], in1=self.mask[d][rs_, :], op=ALU.mult),
                        reads=[pscb, self.b_const], writes=[Pb])
                po, pob = self.pbank.next()
                S.op("pe", lambda e, po=po, P_=P_, tt=tt: e.matmul(po[:, 0:128], self.vtok[:, tt, :], P_[:], start=True, stop=False),
                     reads=[self.b_vtok[tt], Pb], writes=[pob])
                hs = range(2) if fwd else range(1, -1, -1)
                for h in hs:
                    rs_ = slice(h * CH, (h + 1) * CH)
                    cs = slice(tt * 128 + h * CH, tt * 128 + (h + 1) * CH)
                    last = (h == hs[-1])
                    Sp, Spb = self.Sp.next()
                    S.op("dve", lambda e, Sp=Sp, tt=tt, h=h: e.tensor_scalar(
                        out=Sp[:], in0=self.St[:], scalar1=self.em[:, tt, h, 0:1], scalar2=None, op0=ALU.mult),
                        reads=[self.b_S, self.b_em[tt]], writes=[Spb])
                    S.op("pe", lambda e, po=po, Sp=Sp, cs=cs, h=h, last=last: e.matmul(
                        po[:, h * CH:(h + 1) * CH], Sp[:], self.qs[:, cs], start=False, stop=last),
                        reads=[Spb, self.b_qs[tt]], writes=[pob])
                    pd, pdb = self.pbank.next()
                    S.op("pe", lambda e, pd=pd, tt=tt, rs_=rs_: e.matmul(
                        pd[:, 0:128], self.kst[rs_, tt, :], self.vtok[rs_, tt, :], start=True, stop=True),
                        reads=[self.b_kst[tt], self.b_vtok[tt]], writes=[pdb])
                    S.op("dve", lambda e, tt=tt, h=h: e.tensor_scalar(
                        out=self.S2[:], in0=self.St[:], scalar1=self.em[:, tt, h, 2:3], scalar2=None, op0=ALU.mult),
                        reads=[self.b_S, self.b_em[tt]], writes=[self.b_S2])
                    S.op("dve", lambda e, pd=pd, tt=tt, h=h: e.scalar_tensor_tensor(
                        out=self.St[:], in0=pd[:, 0:128], scalar=self.em[:, tt, h, 1:2], in1=self.S2[:],
                        op0=ALU.mult, op1=ALU.add),
                        reads=[pdb, self.b_S2, self.b_em[tt]], writes=[self.b_S])
                if fwd:
                    S.op("act", lambda e, po=po, g=g, sl=sl: e.copy(
                        out=self.ofw[:, g * GT + sl.start:g * GT + sl.stop], in_=po[:, 0:128]),
                        reads=[pob], writes=[self.b_ofw])
                else:
                    S.op("dve", lambda e, po=po, g=g, sl=sl: e.tensor_tensor(
                        out=self.osum[:], in0=po[:, 0:128], in1=self.ofw[:, g * GT + sl.start:g * GT + sl.stop], op=ALU.add),
                        reads=[pob, self.b_ofw], writes=[self.b_osum])
                    S.op("act", lambda e: e.activation(out=self.osq[:], in_=self.osum[:], func=AF.Square),
                         reads=[self.b_osum], writes=[self.b_osq])
                    pn, pnb = self.pbank.next()
                    S.op("pe", lambda e, pn=pn: e.matmul(pn[:, 0:128], self.ones[:], self.osq[:], start=True, stop=True),
                         reads=[self.b_osq, self.b_const], writes=[pnb])
                    S.op("dve", lambda e, pn=pn: e.tensor_scalar(out=self.rs[:], in0=pn[:, 0:128], scalar1=1.0 / 128, scalar2=EPS,
                                                                 op0=ALU.mult, op1=ALU.add),
                         reads=[pnb], writes=[self.b_rs])
                    S.op("act", lambda e: e.activation(out=self.rs[:], in_=self.rs[:], func=AF.Ln), reads=[self.b_rs], writes=[self.b_rs])
                    S.op("act", lambda e: e.activation(out=self.rs[:], in_=self.rs[:], func=AF.Exp, scale=-0.5), reads=[self.b_rs], writes=[self.b_rs])
                    S.op("dve", lambda e: e.scalar_tensor_tensor(out=self.osum[:], in0=self.osum[:], scalar=self.gcol[:, 0:1], in1=self.rs[:],
                                                                 op0=ALU.mult, op1=ALU.mult),
                         reads=[self.b_osum, self.b_rs, self.b_lb], writes=[self.b_osum])
                    S.op("dve", lambda e, oo=oo, sl=sl: e.tensor_tensor(out=oo[:, sl], in0=self.osum[:], in1=self.gT[:, sl], op=ALU.mult),
                         reads=[self.b_osum, self.b_gT], writes=[oob])
            if not fwd:
                S.dma("sp", ood, lambda e, oo=oo, g=g: e.dma_start(out=o_dst_fn(g), in_=oo[:]),
                      reads=[oob], writes=[o_dst_bufs])


HY_BANDS = 16
HY_EMB = 33
HY_W = 64
TWO_PI = 2.0 * math.pi


def hyena_pos_table(L):
    u = np.arange(2 * L)
    lag = np.where(u >= L, u - L, L - u).astype(np.int64)
    lag[0] = 0
    t = (np.linspace(0.0, 1.0, L, dtype=np.float32))[lag]
    bands = np.linspace(1e-4, HY_BANDS - 1, HY_BANDS, dtype=np.float32)[None, :]
    ang = (np.float32(TWO_PI / L) * lag.astype(np.float32)[:, None]) * bands
    pos = np.concatenate([t[:, None], np.cos(ang), -np.sin(ang)], axis=-1).astype(np.float32)
    return np.ascontiguousarray(pos.T), t.astype(np.float32)[None, :].copy()


class HyFilt:
    def __init__(self, cx, pbank, NMAX):
        self.cx = cx
        S = cx.S
        sb = cx.sb
        self.pbank = pbank
        self.fc1 = sb("hf_fc1", [HY_EMB, HY_W], F32)
        self.fc2 = sb("hf_fc2", [HY_W, HY_W], F32)
        self.fc3 = sb("hf_fc3", [HY_W, HY_W], F32)
        self.prm = sb("hf_prm", [HY_W, 8], F32)
        self.sc = sb("hf_sc", [HY_W, 1], F32)
        self.bi = sb("hf_bi", [HY_W, 3], F32)
        self.b_prm = Buf()
        self.d_misc = S.dsem()
        self.pos = cx.ring_sb("hf_pos", 2, [HY_EMB, 512], F32, dma=True)
        self.za = cx.ring_sb("hf_za", 2, [HY_W, 512], F32)
        self.zb = cx.ring_sb("hf_zb", 2, [HY_W, 512], F32, dma=True)
        self.tmp = cx.ring_sb("hf_tmp", 4, [HY_W, 512], F32)
        self.fc4 = sb("hf_fc4", [HY_W, 4, 128], F32)
        self.b_fc4 = Buf()
        self.ndl = sb("hf_ndl", [128, 1], F32)
        self.skc = sb("hf_skc", [128, 2], F32)
        self.tl = cx.ring_sb("hf_tl", 2, [128, 512], F32, dma=True)
        self.zin = cx.ring_sb("hf_zin", 2, [HY_W, 512], F32, dma=True)
        self.win = cx.ring_sb("hf_win", 2, [128, 512], F32)
        self.ff = cx.ring_sb("hf_ff", 2, [128, 512], F32)
        self.filt = sb("hf_filt", [128, NMAX], BF16)
        self.b_filt = Buf()
        self.d_filt = S.dsem()
        self.asum = sb("hf_asum", [128, NMAX // 512], F32)
        self.b_asum = Buf()
        self.inv = sb("hf_inv", [128, 1], F32)

    def load_params(self, fc1, b1, fc2, b2, fc3, b3, freq):
        S = self.cx.S
        B = [self.b_prm]
        col = lambda a: a.rearrange("(c o) -> c o", o=1)
        for dst, src in ((self.fc1[:], fc1), (self.fc2[:], fc2), (self.fc3[:], fc3),
                         (self.prm[:, 0:1], col(b1)), (self.prm[:, 1:2], col(b2)),
                         (self.prm[:, 2:3], col(b3)), (self.prm[:, 3:4], col(freq))):
            S.dma("sp", self.d_misc, lambda e, dst=dst, src=src: e.dma_start(out=dst, in_=src), writes=B)
        S.op("dve", lambda e: e.tensor_scalar(out=self.sc[:], in0=self.prm[:, 3:4], scalar1=1.0 / TWO_PI, scalar2=None, op0=ALU.mult), reads=B, writes=B)
        S.op("dve", lambda e: e.tensor_scalar(out=self.bi[:], in0=self.prm[:, 0:3], scalar1=self.sc[:, 0:1], scalar2=None, op0=ALU.mult), reads=B, writes=B)

    def gen_z(self, pos_dram, z_dram, N, zbuf):
        S = self.cx.S
        for ci in range(N // 512):
            cs = slice(ci * 512, (ci + 1) * 512)
            pt, pb, pd = self.pos.next()
            S.dma("sp", pd, lambda e, pt=pt, cs=cs: e.dma_start(out=pt[:], in_=pos_dram[:, cs]), writes=[pb])
            cur, curb, K = pt, pb, HY_EMB
            for li, wt in enumerate((self.fc1, self.fc2, self.fc3)):
                ps, psb = self.pbank.next()
                S.op("pe", lambda e, ps=ps, wt=wt, cur=cur, K=K: e.matmul(ps[0:HY_W, :], wt[0:K, :], cur[0:K, :], start=True, stop=True),
                     reads=[curb, self.b_prm], writes=[psb])
                tm, tmb = self.tmp.next()
                S.op("dve", lambda e, ps=ps, tm=tm, li=li: e.tensor_scalar(
                    out=tm[:], in0=ps[0:HY_W, :], scalar1=self.sc[:, 0:1], scalar2=self.bi[:, li:li + 1], op0=ALU.mult, op1=ALU.add),
                    reads=[psb, self.b_prm], writes=[tmb])
                mg, mgb = self.tmp.next()
                S.op("dve", lambda e, tm=tm, mg=mg: e.tensor_scalar(out=mg[:], in0=tm[:], scalar1=12582912.0, scalar2=None, op0=ALU.add),
                     reads=[tmb], writes=[mgb])
                S.op("dve", lambda e, tm=tm, mg=mg: e.scalar_tensor_tensor(out=tm[:], in0=mg[:], scalar=-12582912.0, in1=tm[:],
                                                                           op0=ALU.add, op1=ALU.subtract),
                     reads=[tmb, mgb], writes=[tmb])
                if li < 2:
                    nx, nxb = self.za.next()
                    S.op("act", lambda e, tm=tm, nx=nx: e.activation(out=nx[:], in_=tm[:], func=AF.Sin, scale=-(TWO_PI - 1e-5)), reads=[tmb], writes=[nxb])
                    cur, curb, K = nx, nxb, HY_W
                else:
                    zo, zob, zod = self.zb.next()
                    S.op("act", lambda e, tm=tm, zo=zo: e.activation(out=zo[:], in_=tm[:], func=AF.Sin, scale=-(TWO_PI - 1e-5)), reads=[tmb], writes=[zob])
                    S.dma("sp", zod, lambda e, zo=zo, cs=cs: e.dma_start(out=z_dram[:, cs], in_=zo[:]), reads=[zob], writes=[zbuf])

    def load_block(self, fc4_blk, ndl_blk, skip_blk=None):
        S = self.cx.S
        if skip_blk is not None:
            S.dma("sp", self.d_misc, lambda e: e.dma_start(out=self.skc[:], in_=skip_blk.rearrange("n c -> c n"),
                                                           allow_slow_non_contiguous=True), writes=[self.b_fc4])
        self.has_skip = skip_blk is not None
        S.dma("sp", self.d_misc, lambda e: e.dma_start(out=self.fc4[:], in_=fc4_blk), writes=[self.b_fc4])
        S.dma("sp", self.d_misc, lambda e: e.dma_start(out=self.ndl[:], in_=ndl_blk.rearrange("(c o) -> c o", o=1)), writes=[self.b_fc4])

    def gen_filter(self, order, z_dram, zbuf, tl_dram, N, kr_dst, krbuf):
        S = self.cx.S
        L = N // 2
        nch = N // 512
        S.op("dve", lambda e: e.memset(self.asum[:], 0.0), writes=[self.b_asum])
        for ci in range(nch):
            cs = slice(ci * 512, (ci + 1) * 512)
            zi, zib, zid = self.zin.next()
            S.dma("sp", zid, lambda e, zi=zi, cs=cs: e.dma_start(out=zi[:], in_=z_dram[:, cs]), reads=[zbuf], writes=[zib])
            tl, tlb, tld = self.tl.next()
            S.dma("sp", tld, lambda e, tl=tl, cs=cs: e.dma_start(out=tl[:], in_=tl_dram[:, cs].partition_broadcast(128)), writes=[tlb])
            dirn = 1 if ci * 512 < L else 0
            ps, psb = self.pbank.next()
            S.op("pe", lambda e, ps=ps, zi=zi, dirn=dirn: e.matmul(ps[:, :], self.fc4[:, order * 2 + dirn, :], zi[:], start=True, stop=True),
                 reads=[zib, self.b_fc4], writes=[psb])
            wn, wnb = self.win.next()
            S.op("act", lambda e, wn=wn, tl=tl: e.activation(out=wn[:], in_=tl[:], func=AF.Exp, scale=self.ndl[:, 0:1]),
                 reads=[tlb, self.b_fc4], writes=[wnb])
            ff, ffb = self.ff.next()
            S.op("dve", lambda e, ff=ff, ps=ps, wn=wn: e.tensor_tensor(out=ff[:], in0=ps[:, :], in1=wn[:], op=ALU.mult),
                 reads=[psb, wnb], writes=[ffb])
            if ci == 0:
                S.op("dve", lambda e, ff=ff: e.memset(ff[:, 0:1], 0.0), reads=[ffb], writes=[ffb])
            S.op("act", lambda e, ff=ff, cs=cs: e.copy(out=self.filt[:, cs], in_=ff[:]), reads=[ffb], writes=[self.b_filt])
            S.op("dve", lambda e, ff=ff, ci=ci: e.tensor_reduce(out=self.asum[:, ci:ci + 1], in_=ff[:], axis=AX.X, op=ALU.add,
                                                                 apply_absolute_value=True),
                 reads=[ffb], writes=[self.b_asum])
        B = [self.b_asum]
        S.op("dve", lambda e: e.tensor_reduce(out=self.inv[:], in_=self.asum[:, 0:nch], axis=AX.X, op=ALU.add), reads=B, writes=B)
        S.op("dve", lambda e: e.tensor_scalar(out=self.inv[:], in0=self.inv[:], scalar1=EPS, scalar2=None, op0=ALU.add), reads=B, writes=B)
        S.op("dve", lambda e: e.reciprocal(out=self.inv[:], in_=self.inv[:]), reads=B, writes=B)
        S.op("dve", lambda e: e.tensor_scalar(out=self.filt[:, 0:N], in0=self.filt[:, 0:N], scalar1=self.inv[:, 0:1], scalar2=None, op0=ALU.mult),
             reads=[self.b_filt, self.b_asum], writes=[self.b_filt])
        if getattr(self, "has_skip", False):
            S.op("dve", lambda e: e.tensor_scalar(out=self.filt[:, L:L + 1], in0=self.filt[:, L:L + 1],
                                                  scalar1=self.skc[:, order:order + 1], scalar2=None, op0=ALU.add),
                 reads=[self.b_filt, self.b_fc4], writes=[self.b_filt])
        S.dma("sp", self.d_filt, lambda e: e.dma_start(out=kr_dst, in_=self.filt[:, 0:N]), reads=[self.b_filt], writes=[krbuf])


def barrier(S):
    for e in S.eng:
        for k in S.eng:
            if k != e and S.cnt[k] > S.seen[e].get(k, 0):
                S.eng[e].wait_ge(S.sem[k], S.cnt[k])
                S.seen[e][k] = S.cnt[k]
        for d in S.all_dsems:
            if d.cnt > S.seen[e].get(d, 0):
                S.eng[e].wait_ge(d.sem, d.cnt * d.inc)
                S.seen[e][d] = d.cnt


class HyProj:
    def __init__(self, cx, pbank, D, NC3):
        self.cx = cx
        S = cx.S
        sb = cx.sb
        self.pbank = pbank
        self.KT = D // 128
        self.NC3 = NC3
        self.w = sb("hp_w", [128, self.KT, NC3], BF16); self.b_w = Buf(); self.d_w = S.dsem()
        self.hT = cx.ring_sb("hp_hT", 1, [128, self.KT, 512], BF16, dma=True)
        self.ev = cx.ring_sb("hp_ev", 3, [128, 512], F32, dma=True)
        self.cw = sb("hp_cw", [128, 3, NC3], F32)
        self.cb = sb("hp_cb", [128, NC3], F32)
        self.b_c = Buf(); self.d_c = S.dsem()
        self.sh = [cx.ring_sb("hp_sh%d" % i, 2, [128, 512], F32, dma=True) for i in range(3)]
        self.acc = cx.ring_sb("hp_acc", 2, [128, 512], F32, dma=True)
        self.t1 = cx.ring_sb("hp_t1", 2, [128, 512], F32)
        self.zero = sb("hp_zero", [1, NC3], F32); self.b_zero = Buf()

    def load(self, w_dram, cw_dram, cb_dram):
        S = self.cx.S
        wv = w_dram.rearrange("(k p) f -> p k f", p=128)
        half = self.KT // 2
        S.dma("pool", self.d_w, lambda e: e.dma_start(out=self.w[:, 0:half, :], in_=wv[:, 0:half, :]), writes=[self.b_w])
        S.dma("pool", self.d_w, lambda e: e.dma_start(out=self.w[:, half:, :], in_=wv[:, half:, :]), writes=[self.b_w])
        S.dma("sp", self.d_c, lambda e: e.dma_start(out=self.cw[:].rearrange("p a f -> p (a f)"),
                                                   in_=cw_dram.rearrange("a f -> (a f)").partition_broadcast(128)), writes=[self.b_c])
        S.dma("sp", self.d_c, lambda e: e.dma_start(out=self.cb[:], in_=cb_dram.partition_broadcast(128)), writes=[self.b_c])
        S.op("dve", lambda e: e.memset(self.zero[:], 0.0), writes=[self.b_zero])

    def zero_row(self, u_raw, row, ubuf):
        S = self.cx.S
        S.dma("sp", self.d_c, lambda e: e.dma_start(out=u_raw[row:row + 1, :], in_=self.zero[:]), reads=[self.b_zero], writes=[ubuf])

    def project(self, hT_src, u_raw, row0, ubuf):
        S = self.cx.S
        KT = self.KT
        hT, hb, hd = self.hT.next()
        S.dma("sp", hd, lambda e: e.dma_start(out=hT[:], in_=hT_src), writes=[hb])
        for tt in range(4):
            for cb in range(self.NC3 // 512):
                ps, psb = self.pbank.next()

                def mm(e, ps=ps, tt=tt, cb=cb):
                    ins = None
                    for k in range(KT):
                        ins = e.matmul(ps[:, :], hT[:, k, tt * 128:(tt + 1) * 128], self.w[:, k, cb * 512:(cb + 1) * 512],
                                       start=(k == 0), stop=(k == KT - 1))
                    return ins
                S.op("pe", mm, reads=[hb, self.b_w], writes=[psb])
                ev, evb, evd = self.ev.next()
                S.op("act", lambda e, ev=ev, ps=ps: e.copy(out=ev[:], in_=ps[:, :]), reads=[psb], writes=[evb])
                S.dma("sp", evd, lambda e, ev=ev, tt=tt, cb=cb: e.dma_start(
                    out=u_raw[row0 + tt * 128:row0 + (tt + 1) * 128, cb * 512:(cb + 1) * 512], in_=ev[:]),
                    reads=[evb], writes=[ubuf])

    def shortconv(self, u_raw, row0, ubuf, up, trow0, upbuf):
        S = self.cx.S
        for cb in range(self.NC3 // 512):
            cs = slice(cb * 512, (cb + 1) * 512)
            tiles = []
            for i in range(3):
                t, tb, td = self.sh[i].next()
                S.dma("sp", td, lambda e, t=t, i=i, cs=cs: e.dma_start(out=t[:], in_=u_raw[row0 + i - 1:row0 + i - 1 + 128, cs]),
                      reads=[ubuf], writes=[tb])
                tiles.append((t, tb))
            ac, acb, acd = self.acc.next()
            t1, t1b = self.t1.next()
            B = [self.b_c]
            S.op("dve", lambda e, ac=ac, cs=cs: e.tensor_tensor(out=ac[:], in0=tiles[0][0][:], in1=self.cw[:, 0, cs], op=ALU.mult),
                 reads=[tiles[0][1]] + B, writes=[acb])
            S.op("dve", lambda e, t1=t1, cs=cs: e.tensor_tensor(out=t1[:], in0=tiles[1][0][:], in1=self.cw[:, 1, cs], op=ALU.mult),
                 reads=[tiles[1][1]] + B, writes=[t1b])
            S.op("dve", lambda e, ac=ac, t1=t1: e.tensor_tensor(out=ac[:], in0=ac[:], in1=t1[:], op=ALU.add), reads=[acb, t1b], writes=[acb])
            S.op("dve", lambda e, t1=t1, cs=cs: e.tensor_tensor(out=t1[:], in0=tiles[2][0][:], in1=self.cw[:, 2, cs], op=ALU.mult),
                 reads=[tiles[2][1]] + B, writes=[t1b])
            S.op("dve", lambda e, ac=ac, t1=t1: e.tensor_tensor(out=ac[:], in0=ac[:], in1=t1[:], op=ALU.add), reads=[acb, t1b], writes=[acb])
            S.op("dve", lambda e, ac=ac, cs=cs: e.tensor_tensor(out=ac[:], in0=ac[:], in1=self.cb[:, cs], op=ALU.add), reads=[acb] + B, writes=[acb])
            S.dma("sp", acd, lambda e, ac=ac, cs=cs: e.dma_start(out=up[trow0:trow0 + 128, cs], in_=ac[:]), reads=[acb], writes=[upbuf])


class HyConv:
    def __init__(self, cx, pbank, NBMAX, G, NB1):
        self.cx = cx
        S = cx.S
        sb = cx.sb
        self.pbank = pbank
        self.G = G
        BM = NBMAX
        self.J = sb("hc_J", [128, 128], BF16); self.b_J = Buf()
        self.ident = sb("hc_ident", [128, 128], BF16)
        self.d_misc = S.dsem()
        self.yf = sb("hc_yf", [128, BM * G], F32); self.b_yf = Buf(); self.d_yf = S.dsem()
        self.ybf = sb("hc_ybf", [128, BM * G], BF16); self.b_ybf = Buf()
        self.yrev = sb("hc_yrev", [128, BM * G], BF16); self.b_yrev = Buf()
        self.cv = sb("hc_cv", [128, BM * G], F32); self.b_cv = Buf()
        self.ksk = cx.ring_sb("hc_ksk", 2, [128, 128 * NB1 + 2], BF16, dma=True)
        self.oT = cx.ring_sb("hc_oT", 2, [G, 1024], BF16, dma=True)

    def load_consts(self, J_dram, ident_dram):
        S = self.cx.S
        S.dma("sp", self.d_misc, lambda e: e.dma_start(out=self.J[:], in_=J_dram), writes=[self.b_J])
        S.dma("sp", self.d_misc, lambda e: e.dma_start(out=self.ident[:], in_=ident_dram), writes=[self.b_J])

    def run(self, order, nsq, nb, y_src, g_src, kr_t, kr_row0, kr_N, y_dst=None, ydbuf=None, ysbuf=None, gsbuf=None,
            krbuf=None, o_dst=None, obuf=None):
        S = self.cx.S
        G = self.G
        pitch = 2 * nb - 1
        B = (nsq - 1) * pitch + nb
        n = B * G
        v3 = lambda t: t[:, 0:n].rearrange("p (b c) -> p b c", c=G)
        if nsq > 1:
            S.op("dve", lambda e: e.memset(self.ybf[:, 0:n], 0.0), writes=[self.b_ybf])
        for si in range(nsq):
            S.dma("sp", self.d_yf, lambda e, si=si: e.dma_start(out=v3(self.yf)[:, si * pitch:si * pitch + nb, :], in_=y_src(si)),
                  reads=[ysbuf], writes=[self.b_yf])
        for si in range(nsq):
            S.op("act", lambda e, si=si: e.copy(out=v3(self.ybf)[:, si * pitch:si * pitch + nb, :],
                                                in_=v3(self.yf)[:, si * pitch:si * pitch + nb, :]),
                 reads=[self.b_yf], writes=[self.b_ybf])
        for c0 in range(0, n, 512):
            w = min(512, n - c0)
            ps, psb = self.pbank.next()
            S.op("pe", lambda e, ps=ps, c0=c0, w=w: e.matmul(ps[:, 0:w], self.J[:], self.ybf[:, c0:c0 + w], start=True, stop=True),
                 reads=[self.b_J, self.b_ybf], writes=[psb])
            S.op("act", lambda e, ps=ps, c0=c0, w=w: e.copy(out=self.yrev[:, c0:c0 + w], in_=ps[:, 0:w]), reads=[psb], writes=[self.b_yrev])
        for si in range(nsq):
            S.dma("sp", self.d_yf, lambda e, si=si: e.dma_start(out=v3(self.yf)[:, si * pitch:si * pitch + nb, :], in_=g_src(si)),
                  reads=[gsbuf], writes=[self.b_yf])
        yr = v3(self.yrev)
        cvv = v3(self.cv)
        hw = 128 * nb + 1
        for c in range(G):
            halves = []
            for hh in range(2):
                kt, kb, kd = self.ksk.next()
                xs = 0 if hh == 0 else 128 * (nb - 1)
                src = bass.AP(tensor=kr_t, offset=(kr_row0 + c) * kr_N + xs, ap=[[1, 128], [1, hw]])
                S.dma("sp", kd, lambda e, kt=kt, src=src: e.dma_start(out=kt[:, 0:hw], in_=src), reads=[krbuf], writes=[kb])
                halves.append((kt, kb, xs))
            ps, psb = self.pbank.next()

            def mm(e, c=c, ps=ps, halves=halves):
                ins = None
                deltas = [0] + [d_ for d_ in range(-(nb - 1), nb) if d_ != 0]
                for i, dl in enumerate(deltas):
                    kt, _, xs = halves[0] if dl < 0 else halves[1]
                    x0 = 128 * (dl + nb - 1) + 1 - xs
                    a0, a1 = max(0, dl), min(B, B + dl)
                    ins = e.matmul(ps[:, a0:a1], kt[:, x0:x0 + 128], yr[:, a0 - dl:a1 - dl, c],
                                   start=(i == 0), stop=(i == len(deltas) - 1))
                return ins
            S.op("pe", mm, reads=[halves[0][1], halves[1][1], self.b_yrev], writes=[psb])
            S.op("act", lambda e, ps=ps, c=c: e.copy(out=cvv[:, :, c], in_=ps[:, 0:B]), reads=[psb], writes=[self.b_cv])
        for si in range(nsq):
            sl = slice(si * pitch * G, (si * pitch + nb) * G)
            S.op("dve", lambda e, sl=sl: e.tensor_tensor(out=self.cv[:, sl], in0=self.cv[:, sl], in1=self.yf[:, sl], op=ALU.mult),
                 reads=[self.b_cv, self.b_yf], writes=[self.b_cv])
        if order == 0:
            for si in range(nsq):
                S.dma("sp", self.d_yf, lambda e, si=si: e.dma_start(out=y_dst(si), in_=cvv[:, si * pitch:si * pitch + nb, :]),
                      reads=[self.b_cv], writes=[ydbuf])
        else:
            yb = v3(self.ybf)
            for si in range(nsq):
                S.op("act", lambda e, si=si: e.copy(out=yb[:, si * pitch:si * pitch + nb, :], in_=cvv[:, si * pitch:si * pitch + nb, :]),
                     reads=[self.b_cv], writes=[self.b_ybf])
            for si in range(nsq):
                for b0 in range(0, nb, 8):
                    nbk = min(8, nb - b0)
                    ps, psb = self.pbank.next()
                    ptb = ps[:].bitcast(BF16)

                    def tr(e, ptb=ptb, b0=b0, nbk=nbk, si=si):
                        ins = None
                        for j in range(nbk):
                            ins = e.transpose(ptb[0:G, j * 128:(j + 1) * 128], yb[:, si * pitch + b0 + j, :], self.ident[:])
                        return ins
                    S.op("pe", tr, reads=[self.b_ybf, self.b_J], writes=[psb])
                    ot, otb, otd = self.oT.next()
                    S.op("act", lambda e, ot=ot, ptb=ptb, nbk=nbk: e.copy(out=ot[:, 0:nbk * 128], in_=ptb[0:G, 0:nbk * 128]), reads=[psb], writes=[otb])
                    S.dma("sp", otd, lambda e, ot=ot, b0=b0, nbk=nbk, si=si: e.dma_start(
                        out=o_dst(si)[:, b0 * 128:(b0 + nbk) * 128], in_=ot[:, 0:nbk * 128]),
                        reads=[otb], writes=[obuf])


D_MODEL = 4096
D_FF = 11008
NCORE = 8
SEQS = [(0, 16384), (16384, 4096), (20480, 4096)]
T_ALL = 24576
TOKC = T_ALL // NCORE
TT = 512
NH = 4
CC = 512
GC = 64
HY_MIN_DECAY = math.log(1e-2) / 1.5
HY_MAX_DECAY = math.log(1e-2) / 0.3


class Phase:
    def __init__(self, nc, S):
        self.nc, self.S = nc, S

    def __enter__(self):
        self.stack = contextlib.ExitStack()
        self.stack.__enter__()
        cx = Ctx.__new__(Ctx)
        cx.nc, cx.stack, cx.cfg, cx.S = self.nc, self.stack, {}, self.S
        Phase.uid = getattr(Phase, "uid", 0) + 1
        uid = Phase.uid
        cx.sb = lambda name, shape, dt: self.stack.enter_context(self.nc.sbuf_tensor("p%d_%s" % (uid, name), shape, dt))
        return cx

    def __exit__(self, *a):
        barrier(self.S)
        self.S.release_phase()
        return self.stack.__exit__(*a)


def build_program():
    nc = bass.Bass("TRN2", target_bir_lowering=False)
    D, DFF = D_MODEL, D_FF

    def din(name, shape, dt=F32):
        return nc.dram_tensor(name, list(shape), dt, kind="ExternalInput")

    def dint(name, shape, dt):
        return nc.dram_tensor(name, list(shape), dt, kind="Internal")

    x_own = din("x_own", [TOKC, D]).ap()
    norm_mix = din("norm_mix", [2, D]).ap()
    norm_ffn = din("norm_ffn", [2, D]).ap()
    norm_final = din("norm_final", [D]).ap()
    hg_w = din("hg_w", [NH, D, 640]).ap()
    hg_lg = din("hg_lg", [NH, 2, 2, 128]).ap()
    hg_gn = din("hg_gn", [NH, 128]).ap()
    hg_wout = din("hg_wout", [D, D]).ap()
    hy_wout = din("hy_wout", [D, D]).ap()
    hy_w = din("hy_w", [D, 3 * CC]).ap()
    hy_cw = din("hy_cw", [3, 3 * CC]).ap()
    hy_cb = din("hy_cb", [3 * CC]).ap()
    fc1 = din("hy_fc1", [HY_EMB, HY_W]).ap(); b1 = din("hy_b1", [HY_W]).ap()
    fc2 = din("hy_fc2", [HY_W, HY_W]).ap(); b2 = din("hy_b2", [HY_W]).ap()
    fc3 = din("hy_fc3", [HY_W, HY_W]).ap(); b3 = din("hy_b3", [HY_W]).ap()
    freq = din("hy_freq", [HY_W]).ap()
    hy_fc4 = din("hy_fc4", [CC // 128, HY_W, 4, 128]).ap()
    hy_ndl = din("hy_ndl", [CC // 128, 128]).ap()
    hy_skip = din("hy_skip", [CC // 128, 2, 128]).ap()
    ffn_wg = din("ffn_wg", [2, D, DFF]).ap()
    ffn_wu = din("ffn_wu", [2, D, DFF]).ap()
    ffn_wd = din("ffn_wd", [2, DFF, D]).ap()
    c_ident = din("c_ident", [128, 128], BF16).ap()
    c_J = din("c_J", [128, 128], BF16).ap()
    hc = {k: din(k, v.shape).ap() for k, v in hgrn_consts().items()}
    st_info = []
    for nm, L in (("p", 16384), ("s", 4096)):
        st_info.append(dict(L=L, N=2 * L, pos=din("c_pos_" + nm, [HY_EMB, 2 * L]).ap(), tl=din("c_tl_" + nm, [1, 2 * L]).ap(),
                            z=dint("z_" + nm, [HY_W, 2 * L], F32).ap(), kr=dint("kr_" + nm, [2 * CC, 2 * L], BF16)))
    y_own = nc.dram_tensor("y_own", [TOKC, D], F32, kind="ExternalOutput").ap()

    xres = dint("xres", [TOKC, D], F32).ap()
    hT_own = dint("hT_own", [D, TOKC], BF16).ap()
    hT_all = [dint("hT_all%d" % i, [NCORE * D, TOKC], BF16).ap() for i in range(2)]
    o_own = [dint("o_own%d" % i, [CC, T_ALL], BF16).ap() for i in range(2)]
    o_all = [dint("o_all%d" % i, [D, T_ALL], BF16).ap() for i in range(2)]
    u_raw = dint("u_raw", [T_ALL + 2 * len(SEQS), 3 * CC], F32).ap()
    up = dint("up", [T_ALL, 3 * CC], F32).ap()
    y1 = dint("y1", [T_ALL, CC], F32).ap()

    NG = TOKC // TT
    with contextlib.ExitStack() as st:
        cx0 = Ctx(nc, st, {})
        S = cx0.S
        pbank = Ring([cx0.ps("pb%d" % i, [128, 512], F32) for i in range(8)])
        d_cc = S.dsem(inc=1)
        S.live.remove(d_cc)
        pid = nc.sync.partition_id()
        b_hT_own = Buf(); b_hT_all = [Buf(), Buf()]; b_o_own = [Buf(), Buf()]; b_o_all = [Buf(), Buf()]
        xbufs = [[[Buf() for _ in range(D // 512)] for _ in range(TT // 128)] for _ in range(NG)]

        def allgather(src, sb_, dst, db_):
            S.dma("pool", d_cc, lambda e: e.collective_compute(
                "AllGather", ALU.bypass, replica_groups=[list(range(NCORE))], ins=[src], outs=[dst]),
                reads=[sb_], writes=[db_])
            barrier(S)

        def hT_src(hall, gi):
            r, loc = gi // NG, (gi % NG) * TT
            return hall[r * D:(r + 1) * D, :].rearrange("(k p) t -> p k t", p=128)[:, :, loc:loc + TT]

        def own_hT_dst(g):
            return hT_own.rearrange("(k p) t -> p k t", p=128)[:, :, g * TT:(g + 1) * TT]

        with Phase(nc, S) as cx:
            dn = Dense(cx, D, DFF, TT, pbank)
            dn.load_consts(c_ident)
            dn.load_gain(norm_mix[0, :])
            for g in range(NG):
                dn.norm_group(x_own[g * TT:(g + 1) * TT, :], True, own_hT_dst(g), b_hT_own)
        allgather(hT_own, b_hT_own, hT_all[0], b_hT_all[0])

        with Phase(nc, S) as cx:
            hg = Hgrn(cx, D, TT, 16384, pbank)
            hg.load_consts(hc)
            for j in range(NH):
                hg.load_head(hg_w[j], hg_lg[j], hg_gn[j])
                for (s0, L) in SEQS:
                    g0 = s0 // TT
                    src = lambda g, g0=g0: hT_src(hT_all[0], g0 + g)
                    hg.sweep(src, L // TT, True)
                    hg.sweep(src, L // TT, False,
                             o_dst_fn=lambda g, s0=s0, j=j: o_own[0][j * 128:(j + 1) * 128, s0 + g * TT:s0 + (g + 1) * TT],
                             o_dst_bufs=b_o_own[0])
        allgather(o_own[0], b_o_own[0], o_all[0], b_o_all[0])

        def dense_phase(layer, oall, oall_buf, wout, x_src, final):
            with Phase(nc, S) as cx:
                dn = Dense(cx, D, DFF, TT, pbank)
                dn.load_consts(c_ident)
                oview = oall.rearrange("(k p) t -> p k t", p=128)
                for g in range(NG):
                    osrc = oview[:, :, bass.ds(pid * TOKC + g * TT, TT)]
                    gs = slice(g * TT, (g + 1) * TT)
                    dn.b_oT.r.update(oall_buf.w)
                    if final:
                        dn.group(osrc, x_src[gs, :], (xbufs[g] if x_src is xres else None), xres[gs, :], xbufs[g],
                                 wout, ffn_wg[layer], ffn_wu[layer], ffn_wd[layer], norm_ffn[layer, :], norm_final,
                                 y_dst=y_own[gs, :], y_buf=Buf())
                    else:
                        dn.group(osrc, x_src[gs, :], (xbufs[g] if x_src is xres else None), xres[gs, :], xbufs[g],
                                 wout, ffn_wg[layer], ffn_wu[layer], ffn_wd[layer], norm_ffn[layer, :], norm_mix[layer + 1, :],
                                 hT_dst=own_hT_dst(g), hT_dst_buf=b_hT_own)

        dense_phase(0, o_all[0], b_o_all[0], hg_wout, x_own, False)
        allgather(hT_own, b_hT_own, hT_all[1], b_hT_all[1])

        b_u = Buf(); b_up = Buf(); b_y1 = Buf()
        with Phase(nc, S) as cx:
            hp = HyProj(cx, pbank, D, 3 * CC)
            hp.load(hy_w, hy_cw, hy_cb)
            for i, (s0, L) in enumerate(SEQS):
                hp.zero_row(u_raw, s0 + 2 * i, b_u)
                hp.zero_row(u_raw, s0 + L + 2 * i + 1, b_u)
            for i, (s0, L) in enumerate(SEQS):
                for g in range(L // TT):
                    gi = s0 // TT + g
                    hp.project(hT_src(hT_all[1], gi), u_raw, s0 + g * TT + 2 * i + 1, b_u)
            for i, (s0, L) in enumerate(SEQS):
                for b in range(L // 128):
                    hp.shortconv(u_raw, s0 + b * 128 + 2 * i + 1, b_u, up, s0 + b * 128, b_up)
        b_z = Buf(); b_kr = Buf()
        with Phase(nc, S) as cx:
            hf = HyFilt(cx, pbank, 32768)
            hf.load_params(fc1, b1, fc2, b2, fc3, b3, freq)
            for sti in st_info:
                hf.gen_z(sti["pos"], sti["z"], sti["N"], b_z)
            for blk in range(CC // 128):
                hf.load_block(hy_fc4[blk], hy_ndl[blk], hy_skip[blk])
                for sti in st_info:
                    for order in range(2):
                        r0 = order * CC + blk * 128
                        hf.gen_filter(order, sti["z"], b_z, sti["tl"], sti["N"], sti["kr"].ap()[r0:r0 + 128, :], b_kr)
        with Phase(nc, S) as cx:
            nbp = 16384 // 128
            hcv = HyConv(cx, pbank, nbp, GC, nbp)
            hcv.load_consts(c_J, c_ident)
            upv = up.rearrange("(b p) c -> p b c", p=128)
            y1v = y1.rearrange("(b p) c -> p b c", p=128)
            for sti, seqs in ((st_info[0], SEQS[0:1]), (st_info[1], SEQS[1:3])):
                nb = sti["L"] // 128
                for cg in range(CC // GC):
                    def view(v, col, seqs=seqs, nb=nb):
                        return lambda si: v[:, seqs[si][0] // 128:seqs[si][0] // 128 + nb, col:col + GC]
                    hcv.run(0, len(seqs), nb, view(upv, cg * GC), view(upv, CC + cg * GC), sti["kr"], cg * GC, sti["N"],
                            y_dst=view(y1v, cg * GC), ydbuf=b_y1, ysbuf=b_up, gsbuf=b_up, krbuf=b_kr)
                    hcv.run(1, len(seqs), nb, view(y1v, cg * GC), view(upv, 2 * CC + cg * GC), sti["kr"], CC + cg * GC, sti["N"],
                            ysbuf=b_y1, gsbuf=b_up, krbuf=b_kr,
                            o_dst=lambda si, seqs=seqs, cg=cg: o_own[1][cg * GC:(cg + 1) * GC, seqs[si][0]:seqs[si][0] + seqs[si][1]],
                            obuf=b_o_own[1])
        allgather(o_own[1], b_o_own[1], o_all[1], b_o_all[1])

        dense_phase(1, o_all[1], b_o_all[1], hy_wout, xres, True)
        barrier(S)
    return nc


_HOST_CACHE = {}


def _host_consts():
    if "c" in _HOST_CACHE:
        return _HOST_CACHE["c"]
    import ml_dtypes
    c = dict(hgrn_consts())
    c["c_ident"] = np.eye(128, dtype=np.float32).astype(ml_dtypes.bfloat16)
    c["c_J"] = np.ascontiguousarray(np.eye(128, dtype=np.float32)[::-1]).astype(ml_dtypes.bfloat16)
    for nm, L in (("p", 16384), ("s", 4096)):
        pos, tl = hyena_pos_table(L)
        c["c_pos_" + nm] = pos
        c["c_tl_" + nm] = tl
    _HOST_CACHE["c"] = c
    return c


def kernel(x_prompt, x_sample, norm_mix, norm_ffn, norm_final, hg_w_in, hg_lb_logits, hg_out_norm, hg_w_out,
           hy_w_in, hy_conv_w, hy_conv_b, hy_fc1_w, hy_fc1_b, hy_fc2_w, hy_fc2_b, hy_fc3_w, hy_fc3_b, hy_fc4_w,
           hy_sin_freq, hy_skip, hy_w_out, ffn_w_gate, ffn_w_up, ffn_w_down):
    f = lambda a: np.ascontiguousarray(np.asarray(a, dtype=np.float32))
    D = D_MODEL
    x_all = np.concatenate([f(x_prompt).reshape(-1, D), f(x_sample).reshape(-1, D)], axis=0)
    consts = _host_consts()
    deltas = np.abs(np.linspace(HY_MIN_DECAY, HY_MAX_DECAY, D, dtype=np.float32))
    shared = dict(
        norm_mix=f(norm_mix), norm_ffn=f(norm_ffn), norm_final=f(norm_final),
        hg_wout=f(hg_w_out)[0], hy_wout=f(hy_w_out)[0],
        hy_fc1=f(hy_fc1_w)[0], hy_b1=f(hy_fc1_b)[0], hy_fc2=f(hy_fc2_w)[0], hy_b2=f(hy_fc2_b)[0],
        hy_fc3=f(hy_fc3_w)[0], hy_b3=f(hy_fc3_b)[0], hy_freq=f(hy_sin_freq)[0],
        ffn_wg=f(ffn_w_gate), ffn_wu=f(ffn_w_up), ffn_wd=f(ffn_w_down), **consts)
    hgw = f(hg_w_in)[0].reshape(D, 5, D)
    lg = f(hg_lb_logits)
    gn = f(hg_out_norm)[0]
    hyw = f(hy_w_in)[0].reshape(D, 3, D)
    cw = f(hy_conv_w)[0].reshape(3, 3, D)
    cb = f(hy_conv_b)[0].reshape(3, D)
    fc4 = f(hy_fc4_w)[0].reshape(HY_W, 2, 2, D)
    skip = f(hy_skip)[0]
    in_maps = []
    for c in range(NCORE):
        ch = slice(c * CC, (c + 1) * CC)
        m = dict(shared)
        m["x_own"] = np.ascontiguousarray(x_all[c * TOKC:(c + 1) * TOKC])
        hw = hgw[:, [0, 1, 3, 2, 4], :][:, :, ch].reshape(D, 5, NH, 128)
        m["hg_w"] = np.ascontiguousarray(hw.transpose(2, 0, 1, 3).reshape(NH, D, 640))
        m["hg_lg"] = np.ascontiguousarray(lg[:, :, ch].reshape(2, 2, NH, 128).transpose(2, 0, 1, 3))
        m["hg_gn"] = np.ascontiguousarray(gn[ch].reshape(NH, 128))
        m["hy_w"] = np.ascontiguousarray(hyw[:, :, ch].reshape(D, 3 * CC))
        m["hy_cw"] = np.ascontiguousarray(cw[:, :, ch].reshape(3, 3 * CC))
        m["hy_cb"] = np.ascontiguousarray(cb[:, ch].reshape(3 * CC))
        m["hy_fc4"] = np.ascontiguousarray(fc4[:, :, :, ch].reshape(HY_W, 4, CC // 128, 128).transpose(2, 0, 1, 3))
        m["hy_ndl"] = np.ascontiguousarray((-deltas[ch]).reshape(CC // 128, 128))
        m["hy_skip"] = np.ascontiguousarray(skip[:, ch].reshape(2, CC // 128, 128).transpose(1, 0, 2))
        in_maps.append(m)
    if "nc" not in _HOST_CACHE:
        _HOST_CACHE["nc"] = build_program()
    res = run_bass_kernel_spmd(_HOST_CACHE["nc"], in_maps, core_ids=list(range(NCORE)))
    y = np.concatenate([np.asarray(r["y_own"], dtype=np.float32) for r in res.results], axis=0)
    y_prompt = y[:16384].reshape(1, 16384, D)
    y_sample = y[16384:].reshape(2, 4096, D)
    return (y_prompt, y_sample)
```
